# Optimizing a Trainium2 kernel written in Bass

```python
import jax, jax.numpy as jnp
from jax import lax
import numpy as np

D_MODEL = 1024
BATCH = 16
SEQ = 2048
DEPTH = 1

N_META = 16
D_MIX = D_MODEL
HEAD_DIM = 64
D_ATTN = D_MIX // 2
N_ATTN_HEADS = D_ATTN // HEAD_DIM
D_CONV = D_MIX - D_ATTN
N_CONV_GROUPS = D_CONV // HEAD_DIM
CONV_WIDTH = 3
D_FF = 4 * D_MODEL
Q_BLOCK = 128
EPS = 1e-5
NEG_INF = -1e30
D_IN_PROJ = 3 * D_ATTN + N_ATTN_HEADS + 3 * D_CONV

kernel_name = "hymba_fox_shortconv_block"


def rms_norm(x, g):
    xf = x.astype(jnp.float32)
    y = xf * lax.rsqrt(jnp.mean(xf * xf, axis=-1, keepdims=True) + EPS)
    return (y * g.astype(jnp.float32)).astype(x.dtype)


def head_rms_norm(y, n_groups, g):
    b, l, c = y.shape
    yf = y.astype(jnp.float32).reshape(b, l, n_groups, c // n_groups)
    yf = yf * lax.rsqrt(jnp.mean(yf * yf, axis=-1, keepdims=True) + EPS)
    return (yf.reshape(b, l, c) * g.astype(jnp.float32)).astype(y.dtype)


def forgetting_attention(q, k, v, log_f):
    L = q.shape[1]
    cum = jnp.cumsum(log_f.astype(jnp.float32), axis=1).transpose(0, 2, 1)
    scale = HEAD_DIM ** -0.5
    n_blocks = (L - N_META) // Q_BLOCK
    bounds = [(0, N_META)] + [(N_META + i * Q_BLOCK, N_META + (i + 1) * Q_BLOCK)
                              for i in range(n_blocks)]
    outs = []
    for lo, hi in bounds:
        qb, kb, vb = q[:, lo:hi], k[:, :hi], v[:, :hi]
        s = jnp.einsum('bqhd,bkhd->bhqk', qb, kb,
                       preferred_element_type=jnp.float32) * scale
        decay = cum[:, :, lo:hi, None] - cum[:, :, None, :hi]
        causal = jnp.arange(hi)[None, :] <= jnp.arange(lo, hi)[:, None]
        s = jnp.where(causal, s + decay, NEG_INF)
        p = jax.nn.softmax(s, axis=-1)
        outs.append(jnp.einsum('bhqk,bkhd->bqhd', p.astype(vb.dtype), vb))
    return jnp.concatenate(outs, axis=1)


def causal_depthwise_conv(u, w):
    L = u.shape[1]
    up = jnp.pad(u, ((0, 0), (CONV_WIDTH - 1, 0), (0, 0)))
    y = w[0] * up[:, 0:L]
    for kk in range(1, CONV_WIDTH):
        y = y + w[kk] * up[:, kk:kk + L]
    return y


def setup_inputs(seed: int = 0) -> dict:
    key = jax.random.key(seed)
    ks = jax.random.split(key, 14)
    f32 = jnp.float32
    x = jax.random.normal(ks[0], (BATCH, SEQ, D_MODEL), f32)
    meta_tokens = jax.random.normal(ks[1], (N_META, D_MODEL), f32)
    norm_mix_g = 1.0 + 0.02 * jax.random.normal(ks[2], (DEPTH, D_MODEL), f32)
    w_in = jax.random.normal(ks[3], (DEPTH, D_MODEL, D_IN_PROJ), f32) * D_MODEL ** -0.5
    b_f = (jnp.linspace(1.0, 6.0, N_ATTN_HEADS, dtype=f32)[None, :]
           + 0.1 * jax.random.normal(ks[4], (DEPTH, N_ATTN_HEADS), f32))
    conv_w = jax.random.normal(ks[5], (DEPTH, CONV_WIDTH, D_CONV), f32) * CONV_WIDTH ** -0.5
    out_norm_g = 1.0 + 0.02 * jax.random.normal(ks[6], (DEPTH, D_MIX), f32)
    w_out = jax.random.normal(ks[7], (DEPTH, D_MIX, D_MODEL), f32) * D_MIX ** -0.5
    norm_mlp_g = 1.0 + 0.02 * jax.random.normal(ks[8], (DEPTH, D_MODEL), f32)
    w_ff1 = jax.random.normal(ks[9], (DEPTH, D_MODEL, D_FF), f32) * D_MODEL ** -0.5
    w_ff2 = jax.random.normal(ks[10], (DEPTH, D_FF, D_MODEL), f32) * D_FF ** -0.5
    final_norm_g = 1.0 + 0.02 * jax.random.normal(ks[11], (D_MODEL,), f32)
    return {"x": x, "meta_tokens": meta_tokens, "norm_mix_g": norm_mix_g, "w_in": w_in,
            "b_f": b_f, "conv_w": conv_w, "out_norm_g": out_norm_g, "w_out": w_out,
            "norm_mlp_g": norm_mlp_g, "w_ff1": w_ff1, "w_ff2": w_ff2,
            "final_norm_g": final_norm_g}


def reference(x, meta_tokens, norm_mix_g, w_in, b_f, conv_w, out_norm_g, w_out,
              norm_mlp_g, w_ff1, w_ff2, final_norm_g):
    B = x.shape[0]
    meta = jnp.broadcast_to(meta_tokens.astype(x.dtype)[None], (B, N_META, D_MODEL))
    h = jnp.concatenate([meta, x], axis=1)
    L = h.shape[1]
    split_at = [D_ATTN, 2 * D_ATTN, 3 * D_ATTN, 3 * D_ATTN + N_ATTN_HEADS,
                3 * D_ATTN + N_ATTN_HEADS + D_CONV, 3 * D_ATTN + N_ATTN_HEADS + 2 * D_CONV]
    for layer in range(DEPTH):
        xn = rms_norm(h, norm_mix_g[layer])
        proj = jnp.einsum('bld,de->ble', xn, w_in[layer])
        q, k, v, f_logit, b_gate, c_gate, u = jnp.split(proj, split_at, axis=-1)
        log_f = jax.nn.log_sigmoid(f_logit.astype(jnp.float32) + b_f[layer].astype(jnp.float32))
        hs = (B, L, N_ATTN_HEADS, HEAD_DIM)
        y_attn = forgetting_attention(q.reshape(hs), k.reshape(hs), v.reshape(hs), log_f)
        y_attn = y_attn.reshape(B, L, D_ATTN)
        y_conv = b_gate * causal_depthwise_conv(c_gate * u, conv_w[layer].astype(u.dtype))
        y = jnp.concatenate([head_rms_norm(y_attn, N_ATTN_HEADS, out_norm_g[layer, :D_ATTN]),
                             head_rms_norm(y_conv, N_CONV_GROUPS, out_norm_g[layer, D_ATTN:])],
                            axis=-1)
        h = h + jnp.einsum('ble,ed->bld', y, w_out[layer])
        hn = rms_norm(h, norm_mlp_g[layer])
        a = jnp.square(jax.nn.relu(jnp.einsum('bld,df->blf', hn, w_ff1[layer])))
        h = h + jnp.einsum('blf,fd->bld', a, w_ff2[layer])
    h = rms_norm(h, final_norm_g)
    return h[:, N_META:]
```

```python
import bisect
import contextlib
import numpy as np
import concourse.bass as bass
import concourse.mybir as mybir
from concourse.bass_utils import run_bass_kernel_spmd

F32 = mybir.dt.float32
BF16 = mybir.dt.bfloat16
AF = mybir.ActivationFunctionType
ALU = mybir.AluOpType
AX = mybir.AxisListType

NCORES = 8
NSEQ = 2
SEQ = 2048
D = 1024
NMETA = 16
LTOT = NMETA + SEQ
H = 8
HD = 64
DIN = 3080
DFF = 4096
EPS = 1e-5
G1 = 512
NG1 = SEQ // G1
G2 = 256
COL_Q, COL_K, COL_V, COL_F, COL_B, COL_C, COL_U = 0, 512, 1024, 1536, 1544, 2056, 2568


class Buf:
    __slots__ = ("name", "lw", "rd", "sem", "semv", "excl")

    def __init__(self, name):
        self.name = name
        self.excl = False
        self.lw = None
        self.rd = {}
        self.sem = None
        self.semv = 0


class Sched:
    ENGS = ("pe", "act", "dve", "pool", "sp")

    def __init__(self, nc, es, plan):
        self.nc = nc
        self.es = es
        self.emit = plan is not None
        self.plan = plan
        self.h = dict(pe=nc.tensor, act=nc.scalar, dve=nc.vector, pool=nc.gpsimd, sp=nc.sync)
        self.cnt = {e: 0 for e in self.ENGS}
        self.marked = {e: set() for e in self.ENGS}
        self.waited = {e: {} for e in self.ENGS}
        self.sem = {}
        self.tracks = []
        self.stopped = False
        if self.emit:
            for e in self.ENGS:
                self.sem[e] = es.enter_context(nc.semaphore("s_" + e))

    def buf(self, name):
        return Buf(name)

    def track(self, name):
        b = Buf(name)
        if self.emit:
            b.sem = self.es.enter_context(self.nc.semaphore("d_" + name))
        self.tracks.append(b)
        return b

    def _rank(self, e, i):
        lst = self.plan[e]
        k = bisect.bisect_left(lst, i)
        assert k < len(lst) and lst[k] == i, (e, i)
        return k + 1

    def _wait(self, eng, deps):
        w = self.waited[eng]
        for d in deps:
            if d[0] == "e":
                _, e2, i = d
                if e2 == eng and eng == "pe":
                    continue
                if w.get(e2, 0) >= i:
                    continue
                w[e2] = i
                if self.emit:
                    self.h[eng].wait_ge(self.sem[e2], self._rank(e2, i))
                else:
                    self.marked[e2].add(i)
            else:
                b = d[1]
                v = b.semv
                key = ("d", id(b))
                if w.get(key, 0) >= v:
                    continue
                w[key] = v
                if self.emit:
                    self.h[eng].wait_ge(b.sem, v)

    @staticmethod
    def _deps(reads, writes, eng=None):
        deps = []
        for b in reads:
            if b.lw is not None:
                deps.append(b.lw)
            if b.excl:
                deps.extend(tok for key, tok in b.rd.items() if key != eng)
        for b in writes:
            if b.lw is not None:
                deps.append(b.lw)
            deps.extend(b.rd.values())
        return deps

    @staticmethod
    def _update(tok, key, reads, writes):
        for b in writes:
            b.lw = tok
            b.rd = {}
        for b in reads:
            if b not in writes:
                b.rd[key] = tok

    def op(self, eng, fn, reads=(), writes=()):
        if self.stopped:
            return
        self._wait(eng, self._deps(reads, writes, eng))
        idx = self.cnt[eng] + 1
        self.cnt[eng] = idx
        if self.emit:
            inst = fn(self.h[eng])
            if idx in self.planset[eng]:
                inst.then_inc(self.sem[eng], 1)
        self._update(("e", eng, idx), eng, reads, writes)

    def dma(self, q, out, in_, track, reads=(), writes=(), **kw):
        if self.stopped:
            return
        deps = []
        for b in reads:
            if b.lw is not None:
                deps.append(b.lw)
        for b in writes:
            if b.lw is not None and not (b.lw[0] == "d" and b.lw[1] is track):
                deps.append(b.lw)
            deps.extend(b.rd.values())
        self._wait(q, deps)
        track.semv += 16
        if self.emit:
            self.h[q].dma_start(out=out, in_=in_, **kw).then_inc(track.sem, 16)
        self._update(("d", track), ("d", id(track)), reads, writes)

    def barrier(self):
        if self.stopped:
            return
        for e in self.ENGS:
            deps = [("e", e2, self.cnt[e2]) for e2 in self.ENGS if e2 != e and self.cnt[e2] > 0]
            deps += [("d", t) for t in self.tracks if t.semv > 0]
            self._wait(e, deps)

    def finish(self):
        self._wait("sp", [("d", t) for t in self.tracks if t.semv > 0])
        self._wait("sp", [("e", e2, self.cnt[e2]) for e2 in self.ENGS if e2 != "sp" and self.cnt[e2] > 0])


DBG_STOP = None
DBG_CNT = 1


def program(nc, S):
    es = S.es

    ckc = [0]

    def _ck(n):
        if DBG_STOP is not None and DBG_STOP == n:
            ckc[0] += 1
            if ckc[0] == DBG_CNT:
                S.stopped = True

    def dram_in(name, shape):
        return nc.dram_tensor(name, list(shape), F32, kind="ExternalInput").ap()

    x = dram_in("x", [NSEQ, SEQ, D])
    meta = dram_in("meta_tokens", [NMETA, D])
    g_mix = dram_in("norm_mix_g", [1, D])
    w_in = dram_in("w_in", [D, DIN])
    b_f = dram_in("b_f", [1, H])
    conv_w = dram_in("conv_w", [3, 512])
    g_out = dram_in("out_norm_g", [1, D])
    w_out = dram_in("w_out", [D, D])
    g_mlp = dram_in("norm_mlp_g", [1, D])
    w_ff1 = dram_in("w_ff1", [D, DFF])
    w_ff2 = dram_in("w_ff2", [DFF, D])
    g_fin = dram_in("final_norm_g", [1, D])
    c_ident = dram_in("c_ident", [128, 128])
    c_negtri = dram_in("c_negtri", [128, 128])
    c_tri8 = dram_in("c_tri8", [128, 128])
    c_bd = dram_in("c_bd", [128, 128])
    out = nc.dram_tensor("out", [NSEQ, SEQ, D], F32, kind="ExternalOutput").ap()
    h1s = nc.dram_tensor("h1s", [NSEQ * SEQ, D], F32, kind="ExternalOutput").ap()

    def sb(stack, name, shape, dt):
        return stack.enter_context(nc.sbuf_tensor(name, list(shape), dt))

    pb = [es.enter_context(nc.psum_tensor("pb%d" % i, [128, 512], F32)) for i in range(8)]
    pbB = [S.buf("pb%d" % i) for i in range(8)]
    for b_ in pbB:
        b_.excl = True

    ident = sb(es, "ident", [128, 128], BF16)
    ssq = sb(es, "ssq", [128, 4], F32)
    junk = sb(es, "junk", [128, 1024], BF16)
    t_const = S.track("const")
    t_const_sw = S.track("const_sw")
    B_const = S.buf("constbuf")
    B_csw = S.buf("constbuf_sw")
    S.dma("pool", ident[:], c_ident[:, :], t_const_sw, writes=[B_csw])

    B_junk = S.buf("junk")
    B_ssq = S.buf("ssq")

    def rms_rstd(src_ap, n, ncols, B_src, rstd_ap, B_rstd):
        S.op("act", lambda e: e.activation(out=junk[:n, 0:ncols], in_=src_ap, func=AF.Square,
                                           accum_out=ssq[:n, 0:1]),
             reads=[B_src], writes=[B_junk, B_ssq])
        S.op("act", lambda e: e.activation(out=ssq[:n, 1:2], in_=ssq[:n, 0:1], func=AF.Ln,
                                           scale=1.0 / ncols, bias=EPS),
             reads=[B_ssq], writes=[B_ssq])
        S.op("act", lambda e: e.activation(out=rstd_ap, in_=ssq[:n, 1:2], func=AF.Exp, scale=-0.5),
             reads=[B_ssq], writes=[B_rstd])

    with contextlib.ExitStack() as p1:
        w_in_sb = sb(p1, "w_in_sb", [128, 8, DIN], BF16)
        w_out_sb = sb(p1, "w_out_sb", [128, 8, D], BF16)
        KA = sb(p1, "KA", [70, H, LTOT], BF16)
        QA = sb(p1, "QA", [70, H, G1], BF16)
        VA = sb(p1, "VA", [128, 1 + SEQ // 128, H, 65], BF16)
        xt = [sb(p1, "xt%d" % j, [128, D], F32) for j in range(4)]
        xn = [sb(p1, "xn%d" % j, [128, D], BF16) for j in range(2)]
        xnT = sb(p1, "xnT", [128, 8, G1], BF16)
        yT = sb(p1, "yT", [128, 8, G1], BF16)
        CU = sb(p1, "CU", [128, 4, G1 + 2], F32)
        halo0 = sb(p1, "halo0", [128, 4, 2], F32)
        c_m = sb(p1, "c_m", [128, NMETA], F32)
        acc = sb(p1, "acc", [128, G1], F32)
        ysq = sb(p1, "ysq", [128, G1], F32)
        rs = sb(p1, "rs", [128, G1], F32)
        PT = [sb(p1, "PT%d" % i, [128, 512], BF16) for i in range(2)]
        osq = acc[:, :].rearrange("p (h d) -> p h d", h=H)
        otmp = ysq[:, :].rearrange("p (h d) -> p h d", h=H)
        yat = sb(p1, "yat", [128, 512], BF16)
        negtri = sb(p1, "negtri", [128, 128], BF16)
        tri8 = sb(p1, "tri8", [128, 128], F32)
        bd = sb(p1, "bd", [128, 128], F32)
        gmix_b = sb(p1, "gmix_b", [128, D], F32)
        gattn_b = sb(p1, "gattn_b", [128, 512], F32)
        bf_b = sb(p1, "bf_b", [128, H], F32)
        gconv = sb(p1, "gconv", [128, 4], F32)
        cw = sb(p1, "cw", [128, 4, 3], F32)
        small = sb(p1, "small", [128, 64], F32)
        lf = sb(p1, "lf", [128, 4, H], F32)
        C8 = sb(p1, "C8", [8, G1 + 1], F32)
        C8m = sb(p1, "C8m", [8, NMETA], F32)
        R1 = rs[0:8, :]
        R2 = ysq[0:8, :]
        AUGQ = sb(p1, "AUGQ", [8, 6, G1], BF16)
        AUGK = sb(p1, "AUGK", [8, 6, G1], BF16)

        B_win = S.buf("w_in"); B_wout = S.buf("w_out")
        B_KAm = S.buf("KAm"); B_QAm = S.buf("QAm")
        B_KAa = [S.buf("KAa%d" % h) for h in range(H)]; B_QAa = [S.buf("QAa%d" % h) for h in range(H)]
        B_VA = S.buf("VA")
        B_xt = [S.buf("xt%d" % j) for j in range(4)]
        B_xn = [S.buf("xn%d" % j) for j in range(2)]
        B_xnT = S.buf("xnT")
        B_yT = [S.buf("yT%d" % c) for c in range(8)]
        B_CU = [S.buf("CU%d" % c) for c in range(4)]
        B_halo0 = S.buf("halo0")
        B_csb = S.buf("c_m"); B_acc = S.buf("acc"); B_ysq = S.buf("ysq"); B_rs = S.buf("rs")
        B_PT = [S.buf("PT0"), S.buf("PT1")]
        B_osq = B_acc; B_otmp = B_ysq; B_yat = S.buf("yat")
        B_small = S.buf("small"); B_lf = S.buf("lf")
        B_C8 = S.buf("C8"); B_C8m = S.buf("C8m"); B_R1 = B_rs; B_R2 = B_ysq
        B_AUGQ = S.buf("AUGQ"); B_AUGK = S.buf("AUGK")
        t_xt = [S.track("xt%d" % j) for j in range(4)]
        t_win = S.track("win"); t_wout = S.track("wout")
        t_qa = S.track("qa"); t_ka = S.track("ka")

        A_banks = [6, 7]
        a_rot = [0]

        def nextA():
            i = A_banks[a_rot[0] % 2]
            a_rot[0] += 1
            return i

        S.dma("pool", negtri[:], c_negtri[:, :], t_const_sw, writes=[B_csw])
        S.dma("sp", tri8[:], c_tri8[:, :], t_const, writes=[B_const])
        S.dma("sp", bd[:], c_bd[:, :], t_const, writes=[B_const])
        S.dma("sp", gmix_b[:], g_mix.partition_broadcast(128), t_const, writes=[B_const])
        S.dma("sp", gattn_b[:], g_out[:, 0:512].partition_broadcast(128), t_const, writes=[B_const])
        S.dma("sp", bf_b[:], b_f.partition_broadcast(128), t_const, writes=[B_const])
        for ec in range(4):
            S.dma("sp", gconv[:, ec:ec + 1],
                  g_out[0:1, 512 + ec * 128:512 + (ec + 1) * 128].rearrange("o p -> p o"),
                  t_const, writes=[B_const], allow_slow_non_contiguous=True)
            for k in range(3):
                S.dma("sp", cw[:, ec, k:k + 1],
                      conv_w[k:k + 1, ec * 128:(ec + 1) * 128].rearrange("o p -> p o"),
                      t_const, writes=[B_const], allow_slow_non_contiguous=True)
        S.dma("sp", xt[0][0:NMETA, :], meta[:, :], t_xt[0], writes=[B_xt[0]])
        for c in range(8):
            S.dma("pool", w_in_sb[:, c, :], w_in[c * 128:(c + 1) * 128, :], t_win, writes=[B_win],
                  max_dma_last_dim=4096)
        for c in range(8):
            S.dma("pool", w_out_sb[:, c, :], w_out[c * 128:(c + 1) * 128, :], t_wout, writes=[B_wout],
                  max_dma_last_dim=4096)

        S.op("pool", lambda e: e.memset(AUGQ[:, 3:6, :], 1.0), writes=[B_AUGQ])
        S.op("pool", lambda e: e.memset(AUGK[:, 0:3, :], 1.0), writes=[B_AUGK])
        S.op("pool", lambda e: e.memset(VA[:, :, :, 64:65], 1.0), writes=[B_VA])

        _ck(1)

        def norm_tile(src, B_src, n, gb, dst, B_dst):
            rms_rstd(src[:n, :], n, D, B_src, small[:n, 0:1], B_small)
            S.op("dve", lambda e: e.scalar_tensor_tensor(out=dst[:n, :], in0=src[:n, :], scalar=small[:n, 0:1],
                                                         in1=gb[:n, :], op0=ALU.mult, op1=ALU.mult),
                 reads=[B_src, B_small, B_const], writes=[B_dst])

        def transpose8(src, B_src, n, dstT, B_dstT, off, evac_eng):
            bi = nextA()
            pv = pb[bi][:].bitcast(BF16)
            for c in range(8):
                S.op("pe", lambda e, c=c: e.transpose(out=pv[:, c * 128:c * 128 + n], in_=src[:n, c * 128:(c + 1) * 128],
                                                      identity=ident[:n, :n]),
                     reads=[B_src, B_csw], writes=[pbB[bi]])
            pv3 = pv.rearrange("p (c t) -> p c t", c=8)
            if evac_eng == "act":
                S.op("act", lambda e: e.activation(out=dstT[:, :, off:off + n], in_=pv3[:, :, 0:n], func=AF.Copy),
                     reads=[pbB[bi]], writes=[B_dstT])
            else:
                S.op("dve", lambda e: e.tensor_copy(out=dstT[:, :, off:off + n], in_=pv3[:, :, 0:n]),
                     reads=[pbB[bi]], writes=[B_dstT])

        def proj_fm(col0, m, n, rhsT, evac):
            bi = nextA()
            for c in range(8):
                S.op("pe", lambda e, c=c: e.matmul(pb[bi][0:m, 0:n], lhsT=w_in_sb[:, c, col0:col0 + m],
                                                   rhs=rhsT[:, c, 0:n], start=(c == 0), stop=(c == 7)),
                     reads=[B_win, B_xnT], writes=[pbB[bi]])
            evac(bi)

        def softplus_neg(ps_ap, bi, n, dst_ap):
            S.op("dve", lambda e: e.tensor_tensor(out=small[:n, 8:16], in0=ps_ap, in1=bf_b[:n, :], op=ALU.add),
                 reads=[pbB[bi], B_const], writes=[B_small])
            S.op("act", lambda e: e.activation(out=small[:n, 16:24], in_=small[:n, 8:16], func=AF.Exp, scale=-1.0),
                 reads=[B_small], writes=[B_small])
            S.op("act", lambda e: e.activation(out=dst_ap, in_=small[:n, 16:24], func=AF.Ln, bias=1.0),
                 reads=[B_small], writes=[B_lf])

        def split3(src_ap, n, aq, ak, B_src):
            S.op("pool", lambda e: e.tensor_copy(out=aq[:, 0, 0:n], in_=src_ap), reads=[B_src], writes=[B_AUGQ])
            S.op("pool", lambda e: e.tensor_tensor(out=R1[:, 0:n], in0=src_ap, in1=aq[:, 0, 0:n], op=ALU.subtract),
                 reads=[B_src, B_AUGQ], writes=[B_R1])
            S.op("pool", lambda e: e.tensor_copy(out=aq[:, 1, 0:n], in_=R1[:, 0:n]), reads=[B_R1], writes=[B_AUGQ])
            S.op("pool", lambda e: e.tensor_tensor(out=R2[:, 0:n], in0=R1[:, 0:n], in1=aq[:, 1, 0:n], op=ALU.subtract),
                 reads=[B_R1, B_AUGQ], writes=[B_R2])
            S.op("pool", lambda e: e.tensor_copy(out=aq[:, 2, 0:n], in_=R2[:, 0:n]), reads=[B_R2], writes=[B_AUGQ])
            S.op("pool", lambda e: e.tensor_scalar(out=ak[:, 3:6, 0:n], in0=aq[:, 0:3, 0:n], scalar1=-1.0, scalar2=None,
                                                   op0=ALU.mult),
                 reads=[B_AUGQ], writes=[B_AUGK])

        norm_tile(xt[0], B_xt[0], NMETA, gmix_b, xn[0], B_xn[0])
        transpose8(xn[0], B_xn[0], NMETA, xnT, B_xnT, 0, "dve")
        for h in range(H):
            proj_fm(COL_K + h * HD, HD, NMETA, xnT,
                    lambda bi, h=h: S.op("dve", lambda e: e.tensor_copy(out=KA[0:HD, h, 0:NMETA], in_=pb[bi][0:HD, 0:NMETA]),
                                         reads=[pbB[bi]], writes=[B_KAm]))
        bi = nextA()
        for c in range(8):
            S.op("pe", lambda e, c=c: e.matmul(pb[bi][0:NMETA, 0:512], lhsT=xnT[:, c, 0:NMETA],
                                               rhs=w_in_sb[:, c, COL_V:COL_V + 512], start=(c == 0), stop=(c == 7)),
                 reads=[B_win, B_xnT], writes=[pbB[bi]])
        S.op("dve", lambda e, bi=bi: e.tensor_copy(out=VA[0:NMETA, 0, :, 0:HD],
                                                   in_=pb[bi][0:NMETA, :].rearrange("p (h d) -> p h d", h=H)),
             reads=[pbB[bi]], writes=[B_VA])
        bi = nextA()
        for c in range(8):
            S.op("pe", lambda e, c=c: e.matmul(pb[bi][0:NMETA, 0:H], lhsT=xnT[:, c, 0:NMETA],
                                               rhs=w_in_sb[:, c, COL_F:COL_F + H], start=(c == 0), stop=(c == 7)),
                 reads=[B_win, B_xnT], writes=[pbB[bi]])
        softplus_neg(pb[bi][0:NMETA, 0:H], bi, NMETA, lf[0:NMETA, 0, :])
        bi = nextA()
        S.op("pe", lambda e: e.matmul(pb[bi][0:H, 0:NMETA], lhsT=lf[0:NMETA, 0, :], rhs=tri8[0:NMETA, 0:NMETA],
                                      start=True, stop=True),
             reads=[B_lf, B_const], writes=[pbB[bi]])
        S.op("dve", lambda e, bi=bi: e.tensor_copy(out=C8m[:, :], in_=pb[bi][0:H, 0:NMETA]),
             reads=[pbB[bi]], writes=[B_C8m])
        split3(C8m[:, :], NMETA, AUGQ, AUGK, B_C8m)
        for h in range(H):
            S.dma("pool", KA[64:70, h, 0:NMETA], AUGK[h:h + 1, :, 0:NMETA], t_ka, reads=[B_AUGK], writes=[B_KAa[h]])
        for ec in range(4):
            def evac_c(bi):
                S.op("act", lambda e: e.activation(out=c_m[:, :], in_=pb[bi][:, 0:NMETA], func=AF.Copy),
                     reads=[pbB[bi]], writes=[B_csb])
            proj_fm(COL_C + ec * 128, 128, NMETA, xnT, evac_c)

            def evac_u(bi, ec=ec):
                S.op("dve", lambda e: e.tensor_tensor(out=halo0[:, ec, :], in0=pb[bi][:, NMETA - 2:NMETA],
                                                      in1=c_m[:, NMETA - 2:NMETA], op=ALU.mult),
                     reads=[pbB[bi], B_csb], writes=[B_halo0])
            proj_fm(COL_U + ec * 128, 128, NMETA, xnT, evac_u)

        _ck(2)
        evac_flip = [0]

        def evac_copy(out_ap, in_ap, reads, writes):
            evac_flip[0] ^= 1
            if evac_flip[0]:
                S.op("act", lambda e: e.activation(out=out_ap, in_=in_ap, func=AF.Copy), reads=reads, writes=writes)
            else:
                S.op("dve", lambda e: e.tensor_copy(out=out_ap, in_=in_ap), reads=reads, writes=writes)

        for s in range(NSEQ):
            S.op("pool", lambda e: e.tensor_copy(out=C8[:, 0:1], in_=C8m[:, NMETA - 1:NMETA]),
                 reads=[B_C8m], writes=[B_C8])
            for ec in range(4):
                S.op("pool", lambda e, ec=ec: e.tensor_copy(out=CU[:, ec, 0:2], in_=halo0[:, ec, :]),
                     reads=[B_halo0], writes=[B_CU[ec]])
            for g in range(NG1):
                tok0 = g * G1
                kcol0 = NMETA + tok0
                for j in range(4):
                    S.dma("sp", xt[j][:, :], x[s, tok0 + j * 128:tok0 + (j + 1) * 128, :], t_xt[j], writes=[B_xt[j]])
                for j in range(4):
                    norm_tile(xt[j], B_xt[j], 128, gmix_b, xn[j % 2], B_xn[j % 2])
                    transpose8(xn[j % 2], B_xn[j % 2], 128, xnT, B_xnT, j * 128, "dve" if j % 2 else "act")
                _ck(3)
                for h in range(H):
                    proj_fm(COL_Q + h * HD, HD, G1, xnT,
                            lambda bi, h=h: evac_copy(QA[0:HD, h, :], pb[bi][0:HD, :], [pbB[bi]], [B_QAm]))
                for h in range(H):
                    proj_fm(COL_K + h * HD, HD, G1, xnT,
                            lambda bi, h=h: evac_copy(KA[0:HD, h, kcol0:kcol0 + G1], pb[bi][0:HD, :], [pbB[bi]], [B_KAm]))
                _ck(4)
                for j in range(4):
                    blk = 1 + g * 4 + j
                    bi = nextA()
                    for c in range(8):
                        S.op("pe", lambda e, c=c, j=j, bi=bi: e.matmul(pb[bi][:, 0:512], lhsT=xnT[:, c, j * 128:(j + 1) * 128],
                                                                       rhs=w_in_sb[:, c, COL_V:COL_V + 512],
                                                                       start=(c == 0), stop=(c == 7)),
                             reads=[B_win, B_xnT], writes=[pbB[bi]])
                    evac_copy(VA[:, blk, :, 0:HD], pb[bi][:, :].rearrange("p (h d) -> p h d", h=H), [pbB[bi]], [B_VA])
                    bi = nextA()
                    for c in range(8):
                        S.op("pe", lambda e, c=c, j=j, bi=bi: e.matmul(pb[bi][:, 0:H], lhsT=xnT[:, c, j * 128:(j + 1) * 128],
                                                                       rhs=w_in_sb[:, c, COL_F:COL_F + H],
                                                                       start=(c == 0), stop=(c == 7)),
                             reads=[B_win, B_xnT], writes=[pbB[bi]])
                    softplus_neg(pb[bi][:, 0:H], bi, 128, lf[:, j, :])
                bi = nextA()
                for j in range(4):
                    S.op("pe", lambda e, j=j, bi=bi: e.matmul(pb[bi][0:H, j * 128:(j + 1) * 128], lhsT=lf[:, j, :], rhs=tri8[:, :],
                                                              start=True, stop=True, skip_group_check=True),
                         reads=[B_lf, B_const], writes=[pbB[bi]])
                for j in range(4):
                    S.op("dve", lambda e, j=j, bi=bi: e.tensor_scalar(out=C8[:, 1 + j * 128:1 + (j + 1) * 128],
                                                                      in0=pb[bi][0:H, j * 128:(j + 1) * 128],
                                                                      scalar1=C8[:, j * 128:j * 128 + 1], scalar2=None,
                                                                      op0=ALU.add),
                         reads=[pbB[bi], B_C8], writes=[B_C8])
                split3(C8[:, 1:G1 + 1], G1, AUGQ, AUGK, B_C8)
                S.op("pool", lambda e: e.tensor_copy(out=C8[:, 0:1], in_=C8[:, G1:G1 + 1]), reads=[B_C8], writes=[B_C8])
                for h in range(H):
                    S.dma("pool", QA[64:70, h, :], AUGQ[h:h + 1, :, :], t_qa, reads=[B_AUGQ], writes=[B_QAa[h]])
                    S.dma("pool", KA[64:70, h, kcol0:kcol0 + G1], AUGK[h:h + 1, :, :], t_ka, reads=[B_AUGK], writes=[B_KAa[h]])

                _ck(5)
                for ec in range(4):
                    def evac_c(bi, ec=ec):
                        S.op("act", lambda e: e.activation(out=CU[:, ec, 2:G1 + 2], in_=pb[bi][:, :], func=AF.Copy),
                             reads=[pbB[bi]], writes=[B_CU[ec]])
                    proj_fm(COL_C + ec * 128, 128, G1, xnT, evac_c)

                    def evac_u(bi, ec=ec):
                        S.op("dve", lambda e: e.tensor_tensor(out=CU[:, ec, 2:G1 + 2], in0=pb[bi][:, :], in1=CU[:, ec, 2:G1 + 2],
                                                              op=ALU.mult),
                             reads=[pbB[bi]], writes=[B_CU[ec]])
                    proj_fm(COL_U + ec * 128, 128, G1, xnT, evac_u)

                    def evac_b(bi, ec=ec):
                        S.op("dve", lambda e: e.tensor_scalar(out=acc[:, :], in0=CU[:, ec, 2:G1 + 2], scalar1=cw[:, ec, 2:3],
                                                              scalar2=None, op0=ALU.mult),
                             reads=[B_CU[ec], B_const], writes=[B_acc])
                        S.op("dve", lambda e: e.scalar_tensor_tensor(out=acc[:, :], in0=CU[:, ec, 1:G1 + 1],
                                                                     scalar=cw[:, ec, 1:2], in1=acc[:, :],
                                                                     op0=ALU.mult, op1=ALU.add),
                             reads=[B_CU[ec], B_const], writes=[B_acc])
                        S.op("dve", lambda e: e.scalar_tensor_tensor(out=acc[:, :], in0=CU[:, ec, 0:G1],
                                                                     scalar=cw[:, ec, 0:1], in1=acc[:, :],
                                                                     op0=ALU.mult, op1=ALU.add),
                             reads=[B_CU[ec], B_const], writes=[B_acc])
                        S.op("dve", lambda e: e.tensor_tensor(out=acc[:, :], in0=pb[bi][:, :], in1=acc[:, :], op=ALU.mult),
                             reads=[pbB[bi]], writes=[B_acc])
                        S.op("pool", lambda e: e.tensor_copy(out=CU[:, ec, 0:2], in_=CU[:, ec, G1:G1 + 2]),
                             reads=[], writes=[B_CU[ec]])
                    proj_fm(COL_B + ec * 128, 128, G1, xnT, evac_b)
                    S.op("pool", lambda e: e.tensor_tensor(out=ysq[:, :], in0=acc[:, :], in1=acc[:, :], op=ALU.mult),
                         reads=[B_acc], writes=[B_ysq])
                    bi = nextA()
                    S.op("pe", lambda e, bi=bi: e.matmul(pb[bi][:, :], lhsT=bd[:, :], rhs=ysq[:, :], start=True, stop=True),
                         reads=[B_ysq, B_const], writes=[pbB[bi]])
                    S.op("act", lambda e, bi=bi: e.activation(out=rs[:, :], in_=pb[bi][:, :], func=AF.Ln, bias=EPS),
                         reads=[pbB[bi]], writes=[B_rs])
                    S.op("act", lambda e: e.activation(out=rs[:, :], in_=rs[:, :], func=AF.Exp, scale=-0.5),
                         reads=[], writes=[B_rs])
                    S.op("dve", lambda e, ec=ec: e.scalar_tensor_tensor(out=yT[:, 4 + ec, :], in0=acc[:, :],
                                                                        scalar=gconv[:, ec:ec + 1], in1=rs[:, :],
                                                                        op0=ALU.mult, op1=ALU.mult),
                         reads=[B_acc, B_rs, B_const], writes=[B_yT[4 + ec]])

                _ck(6)
                items = []
                for j in range(4):
                    nb = g * 4 + j + 1
                    blks = list(range(0, nb + 1))
                    for h in range(H):
                        for c0 in range(0, len(blks), 4):
                            items.append((j, h, blks[c0:c0 + 4], nb))

                def emit_S(it, k):
                    j, h, blks, nb = it
                    sbi = k % 2
                    for i, blk in enumerate(blks):
                        nk = NMETA if blk == 0 else 128
                        kc = 0 if blk == 0 else NMETA + (blk - 1) * 128
                        S.op("pe", lambda e, i=i, nk=nk, kc=kc, blk=blk: e.matmul(
                            pb[sbi][0:nk, i * 128:(i + 1) * 128], lhsT=KA[0:70, h, kc:kc + nk],
                            rhs=QA[0:70, h, j * 128:(j + 1) * 128], start=True, stop=(blk != nb), skip_group_check=True),
                             reads=[B_KAm, B_KAa[h], B_QAm, B_QAa[h]], writes=[pbB[sbi]])
                        if blk == nb:
                            S.op("pe", lambda e, i=i: e.matmul(pb[sbi][:, i * 128:(i + 1) * 128], lhsT=ident[:, :],
                                                               rhs=negtri[:, :], start=False, stop=True, skip_group_check=True),
                                 reads=[B_csw], writes=[pbB[sbi]])
                    n = len(blks)
                    lo = 0
                    if blks[0] == 0:
                        S.op("act", lambda e: e.activation(out=PT[sbi][0:NMETA, 0:128], in_=pb[sbi][0:NMETA, 0:128],
                                                           func=AF.Exp, scale=0.125),
                             reads=[pbB[sbi]], writes=[B_PT[sbi]])
                        lo = 1
                    if n > lo:
                        S.op("act", lambda e: e.activation(out=PT[sbi][:, lo * 128:n * 128], in_=pb[sbi][:, lo * 128:n * 128],
                                                           func=AF.Exp, scale=0.125),
                             reads=[pbB[sbi]], writes=[B_PT[sbi]])

                def emit_PV(it, k):
                    j, h, blks, nb = it
                    sbi = k % 2
                    ob = 2 + (j % 2) * 2 + (h // 4)
                    for i, blk in enumerate(blks):
                        nk = NMETA if blk == 0 else 128
                        S.op("pe", lambda e, i=i, nk=nk, blk=blk: e.matmul(
                            pb[ob][:, (h % 4) * 65:(h % 4) * 65 + 65], lhsT=PT[sbi][0:nk, i * 128:(i + 1) * 128],
                            rhs=VA[0:nk, blk, h, :], start=(blk == 0), stop=(blk == nb), skip_group_check=True),
                             reads=[B_PT[sbi], B_VA], writes=[pbB[ob]])

                def post_tile(j):
                    oa = 2 + (j % 2) * 2
                    for half in range(2):
                        o3 = pb[oa + half][:, 0:260].rearrange("p (h d) -> p h d", h=4)
                        S.op("dve", lambda e, half=half, o3=o3: e.reciprocal(out=small[:, 24 + half * 4:28 + half * 4],
                                                                              in_=o3[:, :, 64]),
                             reads=[pbB[oa + half]], writes=[B_small])
                        S.op("act", lambda e, half=half, o3=o3: e.activation(out=osq[:, half * 4:half * 4 + 4, :],
                                                                              in_=o3[:, :, 0:HD], func=AF.Square),
                             reads=[pbB[oa + half]], writes=[B_osq])
                    S.op("dve", lambda e: e.tensor_reduce(out=small[:, 32:40], in_=osq[:, :, :], axis=AX.X, op=ALU.add),
                         reads=[B_osq], writes=[B_small])
                    S.op("dve", lambda e: e.tensor_tensor(out=small[:, 32:40], in0=small[:, 32:40], in1=small[:, 24:32], op=ALU.mult),
                         reads=[], writes=[B_small])
                    S.op("dve", lambda e: e.tensor_tensor(out=small[:, 32:40], in0=small[:, 32:40], in1=small[:, 24:32], op=ALU.mult),
                         reads=[], writes=[B_small])
                    S.op("act", lambda e: e.activation(out=small[:, 40:48], in_=small[:, 32:40], func=AF.Ln,
                                                       scale=1.0 / HD, bias=EPS),
                         reads=[], writes=[B_small])
                    S.op("act", lambda e: e.activation(out=small[:, 40:48], in_=small[:, 40:48], func=AF.Exp, scale=-0.5),
                         reads=[], writes=[B_small])
                    S.op("dve", lambda e: e.tensor_tensor(out=small[:, 48:56], in0=small[:, 40:48], in1=small[:, 24:32], op=ALU.mult),
                         reads=[], writes=[B_small])
                    for half in range(2):
                        o3 = pb[oa + half][:, 0:260].rearrange("p (h d) -> p h d", h=4)
                        S.op("dve", lambda e, half=half, o3=o3: e.tensor_tensor(
                            out=otmp[:, half * 4:half * 4 + 4, :], in0=o3[:, :, 0:HD],
                            in1=small[:, 48 + half * 4:52 + half * 4].unsqueeze(2).to_broadcast([128, 4, HD]), op=ALU.mult),
                             reads=[pbB[oa + half], B_small], writes=[B_otmp])
                    S.op("dve", lambda e: e.tensor_tensor(out=yat[:, :], in0=otmp[:, :, :].rearrange("p h d -> p (h d)"),
                                                          in1=gattn_b[:, :], op=ALU.mult),
                         reads=[B_otmp, B_const], writes=[B_yat])
                    bi = nextA()
                    pv = pb[bi][:].bitcast(BF16)
                    for ec in range(4):
                        S.op("pe", lambda e, ec=ec: e.transpose(out=pv[:, ec * 128:(ec + 1) * 128],
                                                                in_=yat[:, ec * 128:(ec + 1) * 128], identity=ident[:, :]),
                             reads=[B_yat, B_csw], writes=[pbB[bi]])
                    pv3 = pv.rearrange("p (c t) -> p c t", c=8)
                    S.op("act", lambda e: e.activation(out=yT[:, 0:4, j * 128:(j + 1) * 128], in_=pv3[:, 0:4, :], func=AF.Copy),
                         reads=[pbB[bi]], writes=B_yT[0:4])

                def wout_tile(j):
                    for half in range(2):
                        bi = nextA()
                        for ec in range(8):
                            S.op("pe", lambda e, ec=ec, bi=bi: e.matmul(pb[bi][:, :], lhsT=yT[:, ec, j * 128:(j + 1) * 128],
                                                                        rhs=w_out_sb[:, ec, half * 512:(half + 1) * 512],
                                                                        start=(ec == 0), stop=(ec == 7)),
                                 reads=[B_yT[ec], B_wout], writes=[pbB[bi]])
                        S.op("dve", lambda e, bi=bi, half=half: e.tensor_tensor(
                            out=xt[j][:, half * 512:(half + 1) * 512], in0=pb[bi][:, :],
                            in1=xt[j][:, half * 512:(half + 1) * 512], op=ALU.add),
                             reads=[pbB[bi]], writes=[B_xt[j]])
                    r0 = s * SEQ + tok0 + j * 128
                    S.dma("sp", h1s[r0:r0 + 128, :], xt[j][:, :], t_xt[j], reads=[B_xt[j]])

                n_it = len(items)
                last_of_tile = {}
                for k, it in enumerate(items):
                    last_of_tile[it[0]] = k
                emit_S(items[0], 0)
                _ck(61)
                for k in range(n_it):
                    if k + 1 < n_it:
                        emit_S(items[k + 1], k + 1)
                    emit_PV(items[k], k)
                    _ck(62)
                    j = items[k][0]
                    if last_of_tile[j] == k:
                        post_tile(j)
                        _ck(63)
                        wout_tile(j)
                        _ck(64)

    _ck(7)
    S.barrier()
    with contextlib.ExitStack() as p2:
        w1_sb = sb(p2, "w1_sb", [128, 8, DFF], BF16)
        w2_sb = sb(p2, "w2_sb", [128, 32, D], BF16)
        aT = sb(p2, "aT", [128, 32, G2], BF16)
        ht = [sb(p2, "ht%d" % j, [128, D], F32) for j in range(4)]
        hn = [sb(p2, "hn%d" % j, [128, D], BF16) for j in range(2)]
        hnT = sb(p2, "hnT", [128, 8, G2], BF16)
        rl = [sb(p2, "rl%d" % j, [128, G2], F32) for j in range(2)]
        ot = [sb(p2, "ot%d" % j, [128, D], F32) for j in range(2)]
        g2_b = sb(p2, "g2_b", [128, D], F32)
        gf_b = sb(p2, "gf_b", [128, D], F32)
        small2 = sb(p2, "small2", [128, 8], F32)

        B_w1 = [S.buf("w1_%d" % c) for c in range(8)]
        B_w2 = [S.buf("w2_%d" % c) for c in range(32)]
        B_aT = [S.buf("aT%d" % c) for c in range(32)]
        B_ht = [S.buf("ht%d" % j) for j in range(4)]
        B_hn = [S.buf("hn%d" % j) for j in range(2)]
        B_hnT = S.buf("hnT")
        B_rl = [S.buf("rl0"), S.buf("rl1")]
        B_ot = [S.buf("ot0"), S.buf("ot1")]
        B_c2 = S.buf("const2")
        B_small2 = S.buf("small2")
        t_ht = [S.track("ht%d" % j) for j in range(4)]
        t_ot = [S.track("ot%d" % j) for j in range(2)]
        t_w1 = [S.track("w1_%d" % c) for c in range(8)]
        t_w2 = [S.track("w2_%d" % c) for c in range(8)]
        t_c2 = S.track("c2")

        S.dma("sp", g2_b[:], g_mlp.partition_broadcast(128), t_c2, writes=[B_c2])
        S.dma("sp", gf_b[:], g_fin.partition_broadcast(128), t_c2, writes=[B_c2])
        NGRP2 = NSEQ * SEQ // G2

        def load_h(gi):
            for j in range(2):
                sl = (gi % 2) * 2 + j
                r0 = gi * G2 + j * 128
                S.dma("sp", ht[sl][:, :], h1s[r0:r0 + 128, :], t_ht[sl], writes=[B_ht[sl]])

        load_h(0)
        for c in range(8):
            S.dma("pool", w1_sb[:, c, :], w_ff1[c * 128:(c + 1) * 128, :], t_w1[c], writes=[B_w1[c]],
                  max_dma_last_dim=4096)
        for c in range(32):
            S.dma("pool", w2_sb[:, c, :], w_ff2[c * 128:(c + 1) * 128, :], t_w2[c // 4], writes=[B_w2[c]],
                  max_dma_last_dim=4096)

        _ck(8)
        T_banks = [6, 7]
        t_rot = [0]
        for gi in range(NGRP2):
            if gi + 1 < NGRP2:
                load_h(gi + 1)
            sls = [(gi % 2) * 2 + j for j in range(2)]
            for j in range(2):
                sl = sls[j]
                rms_rstd(ht[sl][:, :], 128, D, B_ht[sl], small2[:, 0:1], B_small2)
                S.op("dve", lambda e, sl=sl, j=j: e.scalar_tensor_tensor(out=hn[j][:, :], in0=ht[sl][:, :], scalar=small2[:, 0:1],
                                                                         in1=g2_b[:, :], op0=ALU.mult, op1=ALU.mult),
                     reads=[B_ht[sl], B_small2, B_c2], writes=[B_hn[j]])
                bi = T_banks[t_rot[0] % 2]
                t_rot[0] += 1
                pv = pb[bi][:].bitcast(BF16)
                for c in range(8):
                    S.op("pe", lambda e, c=c, j=j: e.transpose(out=pv[:, c * 128:(c + 1) * 128], in_=hn[j][:, c * 128:(c + 1) * 128],
                                                               identity=ident[:, :]),
                         reads=[B_hn[j], B_csw], writes=[pbB[bi]])
                pv3 = pv.rearrange("p (c t) -> p c t", c=8)
                S.op("dve", lambda e, j=j, pv3=pv3: e.tensor_copy(out=hnT[:, :, j * 128:(j + 1) * 128], in_=pv3[:, :, :]),
                     reads=[pbB[bi]], writes=[B_hnT])

            def ffn1(fc):
                fb = fc % 2
                for c in range(8):
                    S.op("pe", lambda e, c=c: e.matmul(pb[fb][:, 0:G2], lhsT=w1_sb[:, c, fc * 128:(fc + 1) * 128],
                                                       rhs=hnT[:, c, :], start=(c == 0), stop=(c == 7)),
                         reads=[B_w1[c], B_hnT], writes=[pbB[fb]])
                S.op("act", lambda e: e.activation(out=rl[fb][:, :], in_=pb[fb][:, 0:G2], func=AF.Relu),
                     reads=[pbB[fb]], writes=[B_rl[fb]])
                S.op("dve", lambda e: e.tensor_tensor(out=aT[:, fc, :], in0=pb[fb][:, 0:G2], in1=rl[fb][:, :], op=ALU.mult),
                     reads=[pbB[fb], B_rl[fb]], writes=[B_aT[fc]])

            def ffn2(fc):
                for j in range(2):
                    for half in range(2):
                        yb = 2 + j * 2 + half
                        S.op("pe", lambda e, j=j, half=half, yb=yb: e.matmul(
                            pb[yb][:, :], lhsT=aT[:, fc, j * 128:(j + 1) * 128], rhs=w2_sb[:, fc, half * 512:(half + 1) * 512],
                            start=(fc == 0), stop=(fc == 31)),
                             reads=[B_aT[fc], B_w2[fc]], writes=[pbB[yb]])

            ffn1(0)
            for fc in range(32):
                if fc + 1 < 32:
                    ffn1(fc + 1)
                ffn2(fc)
            for j in range(2):
                sl = sls[j]
                for half in range(2):
                    yb = 2 + j * 2 + half
                    S.op("dve", lambda e, sl=sl, half=half, yb=yb: e.tensor_tensor(
                        out=ht[sl][:, half * 512:(half + 1) * 512], in0=pb[yb][:, :],
                        in1=ht[sl][:, half * 512:(half + 1) * 512], op=ALU.add),
                         reads=[pbB[yb]], writes=[B_ht[sl]])
                rms_rstd(ht[sl][:, :], 128, D, B_ht[sl], small2[:, 1:2], B_small2)
                S.op("dve", lambda e, sl=sl, j=j: e.scalar_tensor_tensor(out=ot[j][:, :], in0=ht[sl][:, :], scalar=small2[:, 1:2],
                                                                         in1=gf_b[:, :], op0=ALU.mult, op1=ALU.mult),
                     reads=[B_ht[sl], B_small2, B_c2], writes=[B_ot[j]])
                r0 = gi * G2 + j * 128
                S.dma("sp", out[r0 // SEQ, (r0 % SEQ):(r0 % SEQ) + 128, :], ot[j][:, :], t_ot[j], reads=[B_ot[j]])
        S.finish()


def build_nc():
    nc0 = bass.Bass("TRN2", target_bir_lowering=False)
    with contextlib.ExitStack() as es0:
        S0 = Sched(nc0, es0, None)
        program(nc0, S0)
    plan = {e: sorted(S0.marked[e]) for e in Sched.ENGS}
    nc = bass.Bass("TRN2", target_bir_lowering=False)
    with contextlib.ExitStack() as es:
        S = Sched(nc, es, plan)
        S.planset = {e: set(plan[e]) for e in Sched.ENGS}
        program(nc, S)
    return nc


def _consts():
    ident = np.eye(128, dtype=np.float32)
    kk = np.arange(128)[:, None]
    qq = np.arange(128)[None, :]
    negtri = np.where(kk > qq, np.float32(-1e30), np.float32(0.0)).astype(np.float32)
    tri8 = np.where(kk <= qq, np.float32(-8.0), np.float32(0.0)).astype(np.float32)
    bd = np.zeros((128, 128), np.float32)
    bd[:64, :64] = 1.0 / 64
    bd[64:, 64:] = 1.0 / 64
    return dict(c_ident=ident, c_negtri=negtri, c_tri8=tri8, c_bd=bd)


def kernel(x, meta_tokens, norm_mix_g, w_in, b_f, conv_w, out_norm_g, w_out, norm_mlp_g, w_ff1, w_ff2,
           final_norm_g):
    f = lambda a: np.ascontiguousarray(np.asarray(a, dtype=np.float32))
    x = f(x)
    shared = dict(
        meta_tokens=f(meta_tokens), norm_mix_g=f(norm_mix_g).reshape(1, D), w_in=f(w_in).reshape(D, DIN),
        b_f=f(b_f).reshape(1, H), conv_w=f(conv_w).reshape(3, 512), out_norm_g=f(out_norm_g).reshape(1, D),
        w_out=f(w_out).reshape(D, D), norm_mlp_g=f(norm_mlp_g).reshape(1, D), w_ff1=f(w_ff1).reshape(D, DFF),
        w_ff2=f(w_ff2).reshape(DFF, D), final_norm_g=f(final_norm_g).reshape(1, D))
    shared.update(_consts())
    nc = build_nc()
    in_maps = []
    for c in range(NCORES):
        m = dict(shared)
        m["x"] = np.ascontiguousarray(x[c * NSEQ:(c + 1) * NSEQ])
        in_maps.append(m)
    res = run_bass_kernel_spmd(nc, in_maps, core_ids=list(range(NCORES)))
    return np.concatenate([np.asarray(r["out"], dtype=np.float32) for r in res.results], axis=0)
```
